# Optimizing a Trainium2 kernel written in Bass

```python
import jax
import jax.numpy as jnp
from jax import lax
import numpy as np

D_MODEL = 4096
BATCH = 1
SEQ = 8192
DEPTH = 1

PLE_DIM = 256
EPS = 1e-6
NEG_INF = -1e30

MLA_HEADS = 16
Q_LORA = 1024
KV_LORA = 512
QK_NOPE = 128
QK_ROPE = 64
V_HEAD = 128
MLA_WIDTH = MLA_HEADS * V_HEAD
ROPE_THETA = 10000.0
Q_BLOCK = 128

DIL_HEADS = 16
DIL_HEAD_DIM = 128
DIL_WIDTH = DIL_HEADS * DIL_HEAD_DIM
DIL_PATTERNS = ((128, 1), (512, 4), (2048, 16))
ALIBI_MAX_BIAS = 8.0

MIX_WIDTH = MLA_WIDTH + DIL_WIDTH
IN_SPLITS = (Q_LORA, KV_LORA, QK_ROPE, MLA_WIDTH, DIL_WIDTH, DIL_WIDTH, DIL_WIDTH, DIL_WIDTH)
IN_WIDTH = sum(IN_SPLITS)

kernel_name = 'hybrid_mla_dilated_encoder_layer'


def rms_norm(x, g):
    xf = x.astype(jnp.float32)
    y = xf * lax.rsqrt(jnp.mean(xf * xf, axis=-1, keepdims=True) + EPS)
    return (y * g.astype(jnp.float32)).astype(x.dtype)


def apply_rope(t, positions):
    half = t.shape[-1] // 2
    inv = ROPE_THETA ** (-jnp.arange(half, dtype=jnp.float32) / half)
    ang = positions.astype(jnp.float32)[:, :, None] * inv
    ang = ang.reshape(ang.shape[:2] + (1,) * (t.ndim - 3) + (half,))
    cos, sin = jnp.cos(ang), jnp.sin(ang)
    t1 = t[..., :half].astype(jnp.float32)
    t2 = t[..., half:].astype(jnp.float32)
    return jnp.concatenate([t1 * cos - t2 * sin, t1 * sin + t2 * cos], axis=-1).astype(t.dtype)


def mla_attention(q, k, v):
    B, S, H, Dq = q.shape
    nb = S // Q_BLOCK
    scale = Dq ** -0.5
    qb = q.reshape(B, nb, Q_BLOCK, H, Dq).transpose(1, 0, 2, 3, 4)

    def one_block(qi):
        s = jnp.einsum('bqhd,bkhd->bhqk', qi, k, preferred_element_type=jnp.float32) * scale
        pr = jax.nn.softmax(s, axis=-1)
        return jnp.einsum('bhqk,bkhd->bqhd', pr.astype(v.dtype), v)

    out = lax.map(one_block, qb)
    return out.transpose(1, 0, 2, 3, 4).reshape(B, S, H, v.shape[-1])


def dilated_pattern_attention(q, k, v, slopes, window, dilation):
    B, S, H, D = q.shape
    half = window // (2 * dilation)
    L = S // dilation
    nb = -(-L // half)
    Lp = nb * half

    def to_classes(t):
        t = t.reshape(B, L, dilation, H, D).transpose(0, 2, 1, 3, 4)
        return jnp.pad(t, ((0, 0), (0, 0), (0, Lp - L), (0, 0), (0, 0)))

    def band(t):
        tp = jnp.pad(t, ((0, 0), (0, 0), (half, half), (0, 0), (0, 0)))
        tb = tp.reshape(B, dilation, nb + 2, half, H, D)
        return jnp.concatenate([tb[:, :, :-2], tb[:, :, 1:-1], tb[:, :, 2:]], axis=3)

    qb = to_classes(q).reshape(B, dilation, nb, half, H, D)
    kb = band(to_classes(k))
    vb = band(to_classes(v))

    qj = jnp.arange(Lp).reshape(nb, half)
    kj = jnp.arange(nb)[:, None] * half - half + jnp.arange(3 * half)[None, :]
    delta = jnp.abs(qj[:, :, None] - kj[:, None, :])
    valid = (delta <= half) & (kj[:, None, :] >= 0) & (kj[:, None, :] < L)
    alibi = slopes[None, :, None, None] * (dilation * delta).astype(jnp.float32)[:, None]

    s = jnp.einsum('bcnqhd,bcnkhd->bcnhqk', qb, kb, preferred_element_type=jnp.float32)
    s = s * (DIL_HEAD_DIM ** -0.5) - alibi
    s = jnp.where(valid[:, None], s, NEG_INF)
    m = jnp.max(s, axis=-1, keepdims=True)
    e = jnp.exp(s - m)
    l = jnp.sum(e, axis=-1, keepdims=True)
    o = jnp.einsum('bcnhqk,bcnkhd->bcnqhd', e / l, vb.astype(jnp.float32))
    lse = (m + jnp.log(l))[..., 0]

    o = o.reshape(B, dilation, Lp, H, D)[:, :, :L].transpose(0, 2, 1, 3, 4).reshape(B, S, H, D)
    lse = lse.transpose(0, 1, 2, 4, 3).reshape(B, dilation, Lp, H)[:, :, :L]
    lse = lse.transpose(0, 2, 1, 3).reshape(B, S, H)
    return o, lse


def dilated_mixture_attention(q, k, v):
    H = q.shape[2]
    slopes = jnp.asarray(2.0 ** (-ALIBI_MAX_BIAS * np.arange(1, H + 1) / H), jnp.float32)
    results = [dilated_pattern_attention(q, k, v, slopes, w, d) for w, d in DIL_PATTERNS]
    outs = jnp.stack([r[0] for r in results], axis=0)
    lses = jnp.stack([r[1] for r in results], axis=0)
    weights = jax.nn.softmax(lses, axis=0)
    return jnp.sum(weights[..., None] * outs, axis=0)


def setup_inputs(seed: int = 0) -> dict:
    key = jax.random.key(seed)
    ks = jax.random.split(key, 16)

    def dense(k, shape):
        return jax.random.normal(k, shape, jnp.float32) * shape[-2] ** -0.5

    def gain(k, shape):
        return 1.0 + 0.1 * jax.random.normal(k, shape, jnp.float32)

    return {
        'x': jax.random.normal(ks[0], (BATCH, SEQ, D_MODEL), jnp.float32),
        'p': jax.random.normal(ks[1], (DEPTH, BATCH, SEQ, PLE_DIM), jnp.float32),
        'positions': jnp.broadcast_to(jnp.arange(SEQ, dtype=jnp.int32)[None, :], (BATCH, SEQ)),
        'g_mix': gain(ks[2], (DEPTH, D_MODEL)),
        'w_in': dense(ks[3], (DEPTH, D_MODEL, IN_WIDTH)),
        'g_q_latent': gain(ks[4], (DEPTH, Q_LORA)),
        'w_uq': dense(ks[5], (DEPTH, Q_LORA, MLA_HEADS * (QK_NOPE + QK_ROPE))),
        'g_kv_latent': gain(ks[6], (DEPTH, KV_LORA)),
        'w_ukv': dense(ks[7], (DEPTH, KV_LORA, MLA_HEADS * (QK_NOPE + V_HEAD))),
        'g_out_mla': gain(ks[8], (DEPTH, MLA_WIDTH)),
        'g_out_dil': gain(ks[9], (DEPTH, DIL_WIDTH)),
        'w_out': dense(ks[10], (DEPTH, MIX_WIDTH, D_MODEL)),
        'w_ple': dense(ks[11], (DEPTH, PLE_DIM, D_MODEL)),
        'g_ple': gain(ks[12], (DEPTH, D_MODEL)),
        'w_ple_gate': dense(ks[13], (DEPTH, D_MODEL, D_MODEL)),
        'g_final': gain(ks[14], (D_MODEL,)),
    }


def reference(x, p, positions, g_mix, w_in, g_q_latent, w_uq, g_kv_latent, w_ukv,
              g_out_mla, g_out_dil, w_out, w_ple, g_ple, w_ple_gate, g_final):
    B, S, _ = x.shape
    split_idx = [int(v) for v in np.cumsum(IN_SPLITS)[:-1]]
    for i in range(DEPTH):
        h = rms_norm(x, g_mix[i])
        proj = jnp.einsum('bsd,de->bse', h, w_in[i])
        c_q, c_kv, k_rope, gate_a, q_b, k_b, v_b, gate_b = jnp.split(proj, split_idx, axis=-1)

        q_a = jnp.einsum('bsr,re->bse', rms_norm(c_q, g_q_latent[i]), w_uq[i])
        q_a = q_a.reshape(B, S, MLA_HEADS, QK_NOPE + QK_ROPE)
        q_nope, q_pe = q_a[..., :QK_NOPE], apply_rope(q_a[..., QK_NOPE:], positions)
        kv = jnp.einsum('bsr,re->bse', rms_norm(c_kv, g_kv_latent[i]), w_ukv[i])
        kv = kv.reshape(B, S, MLA_HEADS, QK_NOPE + V_HEAD)
        k_nope, v_a = kv[..., :QK_NOPE], kv[..., QK_NOPE:]
        k_pe = jnp.broadcast_to(apply_rope(k_rope, positions)[:, :, None, :], (B, S, MLA_HEADS, QK_ROPE))
        y_a = mla_attention(jnp.concatenate([q_nope, q_pe], axis=-1),
                            jnp.concatenate([k_nope, k_pe], axis=-1), v_a)
        y_a = rms_norm(y_a.reshape(B, S, MLA_WIDTH), g_out_mla[i]) * jax.nn.silu(gate_a)

        shp = (B, S, DIL_HEADS, DIL_HEAD_DIM)
        y_b = dilated_mixture_attention(q_b.reshape(shp), k_b.reshape(shp), v_b.reshape(shp))
        y_b = rms_norm(y_b.reshape(B, S, DIL_WIDTH).astype(x.dtype), g_out_dil[i]) * jax.nn.silu(gate_b)

        x = x + jnp.einsum('bse,ed->bsd', jnp.concatenate([y_a, y_b], axis=-1), w_out[i])

        ple = jnp.einsum('bsc,cd->bsd', p[i], w_ple[i])
        gate = jax.nn.sigmoid(jnp.einsum('bsd,de->bse', rms_norm(x, g_ple[i]), w_ple_gate[i]))
        x = x + ple * gate
    return rms_norm(x, g_final)
```

```python
import math
import numpy as np
from contextlib import ExitStack
import concourse.bass as bass
import concourse.mybir as mybir
from concourse.bass_utils import run_bass_kernel_spmd

F32 = mybir.dt.float32
BF16 = mybir.dt.bfloat16
I32 = mybir.dt.int32
AF = mybir.ActivationFunctionType
ALU = mybir.AluOpType

NCORES = 8
S = 8192
D = 4096
T = 1024
NT = T // 128
EPS = 1e-6
ROWS = 4160
KPE_ROW = 2048
VA_OFF = 2112 * 1024
NEG = -30000.0
TABW = 2944

C_CQ, C_CKV, C_KR, C_VB, C_GA, C_GB, C_QB, C_KB = 0, 1024, 1536, 1600, 3648, 5696, 7744, 9792
IN_W = 11840


class Reg:
    __slots__ = ("name", "writers", "readers")

    def __init__(self, name=""):
        self.name = name
        self.writers = {}
        self.readers = {}


class Prog:
    ENGS = ("pe", "act", "dve", "pool", "sp")
    NDMA = {"sp": 24, "pool": 12}
    EMAP = {"pe": "tensor", "act": "scalar", "dve": "vector", "pool": "gpsimd", "sp": "sync"}

    SEM_ROTATE = 4000

    def __init__(self, nc, stack):
        self.nc = nc
        self.stack = stack
        self.nsem = 0
        self.keep = []
        self.ops = {e: [] for e in self.ENGS}
        self.count = {e: 0 for e in self.ENGS}
        self.sem = {e: stack.enter_context(nc.semaphore(f"s_{e}")) for e in self.ENGS}
        self.known = {e: {} for e in self.ENGS}
        self.dsem = {q: [stack.enter_context(nc.semaphore(f"d_{q}{i}")) for i in range(n)]
                     for q, n in self.NDMA.items()}
        self.dcount = {q: 0 for q in self.NDMA}

    def _need(self, eng, tok, waits):
        sem, val = tok[0], tok[1]
        k = id(sem)
        if self.known[eng].get(k, 0) >= val:
            return
        self.known[eng][k] = val
        waits.append((sem, val))

    def op(self, eng, fn, reads=(), writes=(), dma=False, own_sem=None, acc=False):
        waits = []
        for r in reads:
            for t in r.writers.values():
                if (not dma) and (not t[3]) and t[2] == eng and eng == "pe":
                    continue
                self._need(eng, t, waits)
        for w in writes:
            for t in w.writers.values():
                if (not dma) and (not t[3]) and t[2] == eng:
                    continue
                self._need(eng, t, waits)
            for t in w.readers.values():
                if (not dma) and (not t[3]) and t[2] == eng:
                    continue
                self._need(eng, t, waits)
        if own_sem is not None:
            sem = own_sem
            tok = (sem, 1, eng, True)
            incv = 1
        elif dma:
            n = self.NDMA[eng]
            slot = self.dcount[eng] % n
            k = self.dcount[eng] // n
            self.dcount[eng] += 1
            sem = self.dsem[eng][slot]
            if k > 0:
                self._need(eng, (sem, 16 * k), waits)
            tok = (sem, 16 * (k + 1), eng, True)
            incv = 16
        else:
            if self.count[eng] >= self.SEM_ROTATE:
                self.nsem += 1
                self.keep.append(self.sem[eng])
                self.sem[eng] = self.stack.enter_context(self.nc.semaphore(f"s_{eng}_r{self.nsem}"))
                self.count[eng] = 0
            self.count[eng] += 1
            sem = self.sem[eng]
            tok = (sem, self.count[eng], eng, False)
            incv = 1
        for r in reads:
            r.readers[id(sem)] = tok
        for w in writes:
            if acc:
                w.writers[id(sem)] = tok
            else:
                w.writers = {id(sem): tok}
                w.readers = {}
        self.ops[eng].append((waits, fn, sem, incv))
        return tok

    def barrier(self):
        toks = []
        for e in self.ENGS:
            if self.count[e] > 0:
                toks.append((self.sem[e], self.count[e], e, False))
        for q, n in self.NDMA.items():
            c = self.dcount[q]
            for slot in range(min(n, c)):
                uses = (c - slot + n - 1) // n
                toks.append((self.dsem[q][slot], 16 * uses, q, True))
        for e in self.ENGS:
            waits = []
            for t in toks:
                self._need(e, t, waits)
            self.ops[e].append((waits, None, None, None))

    def emit(self):
        nc = self.nc
        with nc.Block() as block:
            for e in self.ENGS:
                ops = self.ops[e]

                def body(eng, ops=ops):
                    for waits, fn, sem, incv in ops:
                        for s, v in waits:
                            eng.wait_ge(s, v)
                        if fn is not None:
                            r = fn(eng)
                            if sem is not None:
                                r.then_inc(sem, incv)

                getattr(block, self.EMAP[e])(body)
        self.ops = {e: [] for e in self.ENGS}


class _Stop(Exception):
    pass


def build_program(stop_after=None, debug=()):
    nc = bass.Bass("TRN2", target_bir_lowering=False)
    nemit = [0]

    def din(name, shape, dt=F32):
        return nc.dram_tensor(name, list(shape), dt, kind="ExternalInput")

    def dscr(name, shape, dt):
        if debug and name in debug:
            return nc.dram_tensor(name, list(shape), dt, kind="ExternalOutput")
        return nc.dram_tensor(name, list(shape), dt, kind="Internal")

    x_ext = din("x_ext", [3 * T, D])
    p_in = din("p_in", [T, 256])
    pos_in = din("pos_in", [128, NT], I32)
    w_in = din("w_in", [D, IN_W])
    w_uq = din("w_uq", [1024, 3072])
    w_ukv = din("w_ukv", [512, 4096])
    w_out = din("w_out", [D, D])
    w_ple = din("w_ple", [256, D])
    w_pg = din("w_pg", [D, D])
    g_mix = din("g_mix", [1, D])
    g_ql = din("g_ql", [1, 1024])
    g_kvl = din("g_kvl", [1, 512])
    g_oa = din("g_oa", [128, 16])
    g_ob = din("g_ob", [128, 16])
    g_ple = din("g_ple", [1, D])
    g_fin = din("g_fin", [1, D])
    c_inv8 = din("c_inv8", [1, 256])
    c_tabs = din("c_tabs", [128, TABW])
    c_tln = din("c_tln", [128, TABW])
    c_nslope = din("c_nslope", [1, 16])
    c_edge = din("c_edge", [1, 4])
    c_ident = din("c_ident", [128, 128])
    out_t = nc.dram_tensor("out", [T, D], F32, kind="ExternalOutput")

    X = dscr("xchg", [ROWS, 1024], BF16)
    G = dscr("gath", [NCORES * ROWS, 1024], BF16)
    QnT = dscr("QnT", [2048, T], BF16)
    QpeT = dscr("QpeT", [1024, T], BF16)
    sgT = dscr("sgT", [4096, T], BF16)
    qbT = dscr("qbT", [2048, T], BF16)
    kbT = dscr("kbT", [2048, 3 * T], BF16)
    vbS = dscr("vbS", [3 * T, 2048], BF16)
    x1S = dscr("x1S", [T, D], F32)
    x2S = dscr("x2S", [T, D], F32)

    def AP(t, off, dims):
        return bass.AP(t, off, [list(d) for d in dims])

    def bcast(t, n, p=128):
        return AP(t, 0, [[0, p], [1, n]])

    try:
      with ExitStack() as st:
        P = Prog(nc, st)
        ccsem = st.enter_context(nc.semaphore("ccsem"))
        _emit0 = P.emit

        def _emit():
            _emit0()
            nemit[0] += 1
            if stop_after is not None and nemit[0] >= stop_after:
                raise _Stop()
        P.emit = _emit

        uid = [0]

        def sbt(stack, name, shape, dt):
            uid[0] += 1
            return stack.enter_context(nc.sbuf_tensor(f"{name}_u{uid[0]}", list(shape), dt))

        def pst(stack, name, shape, dt):
            uid[0] += 1
            return stack.enter_context(nc.psum_tensor(f"{name}_u{uid[0]}", list(shape), dt))

        identf = sbt(st, "identf", [128, 128], F32)
        ident = sbt(st, "ident", [128, 128], BF16)
        ones_bf = sbt(st, "ones_bf", [128, 128], BF16)
        ones_f = sbt(st, "ones_f", [128, 128], F32)
        edge = sbt(st, "edge", [128, 4], F32)
        nslope = sbt(st, "nslope", [128, 16], F32)
        goa = sbt(st, "goa", [128, 16], F32)
        gob = sbt(st, "gob", [128, 16], F32)
        r_const = Reg("const")
        r_yT = [[Reg() for _ in range(2)] for _ in range(32)]
        r_ssq = [[Reg() for _ in range(2)] for _ in range(2)]

        P.op("sp", lambda e: e.dma_start(out=identf[:], in_=c_ident.ap()), writes=[r_const], dma=True)
        r_ident = Reg("ident")
        P.op("dve", lambda e: e.tensor_copy(out=ident[:], in_=identf[:]), reads=[r_const], writes=[r_ident])
        r_ones = Reg("ones")
        P.op("pool", lambda e: e.memset(ones_bf[:], 1.0), writes=[r_ones])
        r_onesf = Reg("onesf")
        P.op("pool", lambda e: e.memset(ones_f[:], 1.0), writes=[r_onesf])
        r_edge = Reg("edge")
        P.op("sp", lambda e: e.dma_start(out=edge[:], in_=bcast(c_edge, 4)), writes=[r_edge], dma=True)
        r_nsl = Reg("nsl")
        P.op("sp", lambda e: e.dma_start(out=nslope[:], in_=bcast(c_nslope, 16)), writes=[r_nsl], dma=True)
        r_goa = Reg("goa")
        P.op("sp", lambda e: e.dma_start(out=goa[:], in_=g_oa.ap()), writes=[r_goa], dma=True)
        r_gob = Reg("gob")
        P.op("sp", lambda e: e.dma_start(out=gob[:], in_=g_ob.ap()), writes=[r_gob], dma=True)
        evac_rr = [0]

        def evac_eng():
            evac_rr[0] ^= 1
            return "act" if evac_rr[0] else "dve"

        def copy_op(eng, out_ap, in_ap, reads, writes, acc=False):
            if eng == "act":
                P.op("act", lambda e: e.activation(out=out_ap, in_=in_ap, func=AF.Copy), reads=reads, writes=writes, acc=acc)
            else:
                P.op(eng, lambda e: e.tensor_copy(out=out_ap, in_=in_ap), reads=reads, writes=writes, acc=acc)

        def rms_tm(src_ap, W, gbc_ap, out_ap, junk_ap, stat, r_stat, r_src, r_gbc, r_out, r_junk, out_reads=()):
            P.op("pool", lambda e: e.memset(stat[:, 0:1], 0.0), writes=[r_stat])
            P.op("act", lambda e: e.activation(out=junk_ap, in_=src_ap, func=AF.Square, accum_out=stat[:, 0:1]),
                 reads=[r_src], writes=[r_junk, r_stat])
            P.op("dve", lambda e: e.tensor_scalar(out=stat[:, 1:2], in0=stat[:, 0:1], scalar1=1.0 / W, scalar2=EPS,
                                                  op0=ALU.mult, op1=ALU.add), reads=[r_stat], writes=[r_stat])
            P.op("act", lambda e: e.sqrt(out=stat[:, 2:3], in_=stat[:, 1:2]), reads=[r_stat], writes=[r_stat])
            P.op("dve", lambda e: e.reciprocal(out=stat[:, 3:4], in_=stat[:, 2:3]), reads=[r_stat], writes=[r_stat])
            P.op("dve", lambda e: e.scalar_tensor_tensor(out=out_ap, in0=src_ap, scalar=stat[:, 3:4], in1=gbc_ap,
                                                         op0=ALU.mult, op1=ALU.mult),
                 reads=[r_src, r_stat, r_gbc] + list(out_reads), writes=[r_out])

        def transpose_to(in_tile, nch, cw, psT, r_psT, rr, r_in, dst_fn, r_dst_fn):
            c0 = 0
            while c0 < nch:
                n = min(8, nch - c0)
                b = rr[0] % len(psT)
                rr[0] += 1
                for i in range(n):
                    c = c0 + i
                    P.op("pe", lambda e, c=c, i=i, b=b: e.transpose(psT[b][0:cw, i, :], in_tile[:, c * cw:(c + 1) * cw],
                                                                    ident[:]),
                         reads=[r_in, r_ident], writes=[r_psT[b]])
                copy_op(evac_eng(), dst_fn(c0, n), psT[b][0:cw, 0:n, :], [r_psT[b]], r_dst_fn(c0, n))
                c0 += n

        def w_block_load(wb, r_wb, col0, ncols):
            for g in range(4):
                src = AP(w_in, g * 8 * 128 * IN_W + col0, [[IN_W, 128], [128 * IN_W, 8], [1, ncols]])
                P.op("pool", lambda e, g=g, src=src: e.dma_start(out=wb[:, g * 8:(g + 1) * 8, 0:ncols], in_=src),
                     writes=[r_wb[g]], dma=True)

        def rope_tables(scope, width, sin_t, cos_t, r_tab, posf, r_posf, tag):
            inv = sbt(scope, f"inv_{tag}", [128, width], F32)
            ang = sbt(scope, f"ang_{tag}", [128, width], F32)
            kq = sbt(scope, f"kq_{tag}", [128, width], F32)
            kqi = sbt(scope, f"kqi_{tag}", [128, width], I32)
            r_inv, r_ang, r_kq, r_kqi = Reg(), Reg(), Reg(), Reg()
            TWO_PI = 2.0 * math.pi
            P.op("sp", lambda e: e.dma_start(out=inv[:], in_=AP(c_inv8, 0, [[0, 128], [1, width]])), writes=[r_inv], dma=True)
            for tt in range(NT):
                for which, tab in ((0, sin_t), (1, cos_t)):
                    shift = 0.0 if which == 0 else math.pi / 2.0
                    P.op("dve", lambda e, tt=tt, shift=shift: e.tensor_scalar(
                        out=ang[:], in0=inv[:], scalar1=posf[:, tt:tt + 1], scalar2=shift,
                        op0=ALU.mult, op1=ALU.add), reads=[r_inv, r_posf], writes=[r_ang])
                    P.op("dve", lambda e: e.tensor_scalar(out=kq[:], in0=ang[:], scalar1=1.0 / TWO_PI,
                                                          scalar2=None, op0=ALU.mult), reads=[r_ang], writes=[r_kq])
                    P.op("dve", lambda e: e.tensor_copy(out=kqi[:], in_=kq[:]), reads=[r_kq], writes=[r_kqi])
                    P.op("dve", lambda e: e.tensor_copy(out=kq[:], in_=kqi[:]), reads=[r_kqi], writes=[r_kq])
                    P.op("dve", lambda e: e.scalar_tensor_tensor(out=ang[:], in0=kq[:], scalar=-TWO_PI,
                                                                 in1=ang[:], op0=ALU.mult, op1=ALU.add),
                         reads=[r_kq, r_ang], writes=[r_ang])
                    P.op("dve", lambda e: e.tensor_scalar(out=ang[:], in0=ang[:], scalar1=math.pi, scalar2=-math.pi,
                                                          op0=ALU.min, op1=ALU.max), reads=[r_ang], writes=[r_ang])
                    P.op("act", lambda e, tab=tab, tt=tt: e.activation(out=tab[:, tt, :], in_=ang[:], func=AF.Sin),
                         reads=[r_ang], writes=[r_tab[tt]], acc=(which == 1))

        def projection_pass(pass_id):
            own = (pass_id == 0)
            xrow0 = {0: T, 1: 0, 2: 2 * T}[pass_id]
            win0 = xrow0
            with ExitStack() as ps_:
                if own:
                    cqnT = sbt(ps_, "cqnT", [128, 8, T], BF16)
                    ckvnT = sbt(ps_, "ckvnT", [128, 4, T], BF16)
                    posi = sbt(ps_, "posi", [128, NT], I32)
                    posf = sbt(ps_, "posf", [128, NT], F32)
                    r_cqnT = [Reg() for _ in range(NT)]
                    r_ckvnT = [Reg() for _ in range(NT)]
                    r_posi, r_posf = Reg(), Reg()
                    P.op("sp", lambda e: e.dma_start(out=posi[:], in_=pos_in.ap()), writes=[r_posi], dma=True)
                    P.op("dve", lambda e: e.tensor_copy(out=posf[:], in_=posi[:]), reads=[r_posi], writes=[r_posf])
                sh = ExitStack()
                hT = sbt(sh, f"hT{pass_id}", [128, 32, T], BF16)
                r_hT = [[Reg() for _ in range(4)] for _ in range(NT)]
                with ExitStack() as s1:
                    gbc = sbt(s1, f"gbc{pass_id}", [128, D], F32)
                    xt = [sbt(s1, f"xt{pass_id}_{i}", [128, D], F32) for i in range(2)]
                    xn = [sbt(s1, f"xn{pass_id}_{i}", [128, D], BF16) for i in range(2)]
                    junk = sbt(s1, f"junk{pass_id}", [128, D], BF16)
                    stat = [sbt(s1, f"stat{pass_id}_{i}", [128, 4], F32) for i in range(2)]
                    psT = [pst(s1, f"psT{pass_id}_{i}", [128, 8, 128], BF16) for i in range(2)]
                    r_gbc, r_junk = Reg(), Reg()
                    r_xt = [Reg(), Reg()]
                    r_xn = [Reg(), Reg()]
                    r_stat = [Reg(), Reg()]
                    r_psT = [Reg(), Reg()]
                    rr = [0]
                    P.op("sp", lambda e: e.dma_start(out=gbc[:], in_=bcast(g_mix, D)), writes=[r_gbc], dma=True)
                    for tt in range(NT):
                        b = tt % 2
                        src = AP(x_ext, (xrow0 + tt * 128) * D, [[D, 128], [1, D]])
                        P.op("sp", lambda e, b=b, src=src: e.dma_start(out=xt[b][:], in_=src), writes=[r_xt[b]], dma=True)
                        rms_tm(xt[b][:], D, gbc[:], xn[b][:], junk[:], stat[b], r_stat[b], r_xt[b], r_gbc, r_xn[b], r_junk)
                        transpose_to(xn[b], 32, 128, psT, r_psT, rr, r_xn[b],
                                     lambda c0, n, tt=tt: hT[:, c0:c0 + n, tt * 128:(tt + 1) * 128],
                                     lambda c0, n, tt=tt: [r_hT[tt][c0 // 8]])
                    P.barrier()
                    P.emit()
                with ExitStack() as s2:
                    wb = [sbt(s2, f"wb{pass_id}_{i}", [128, 32, 512], BF16) for i in range(2)]
                    r_wb = [[Reg() for _ in range(4)] for _ in range(2)]
                    psA = [pst(s2, f"psA{pass_id}_{i}", [128, 512], F32) for i in range(3)]
                    r_psA = [Reg() for _ in range(3)]
                    cnt = {"ps": 0, "ob": 0}

                    def tm_matmul(wbi, ncols, tt):
                        b = cnt["ps"] % 3
                        cnt["ps"] += 1
                        for c in range(32):
                            P.op("pe", lambda e, c=c, b=b: e.matmul(psA[b][:, 0:ncols],
                                                                    lhsT=hT[:, c, tt * 128:(tt + 1) * 128],
                                                                    rhs=wb[wbi][:, c, 0:ncols], start=(c == 0), stop=(c == 31)),
                                 reads=[r_hT[tt][c // 8], r_wb[wbi][c // 8]], writes=[r_psA[b]])
                        return b

                    def fm_matmul(wbi, cc, th):
                        b = cnt["ps"] % 3
                        cnt["ps"] += 1
                        for c in range(32):
                            P.op("pe", lambda e, c=c, b=b: e.matmul(psA[b][:, :],
                                                                    lhsT=wb[wbi][:, c, cc * 128:(cc + 1) * 128],
                                                                    rhs=hT[:, c, th * 512:(th + 1) * 512],
                                                                    start=(c == 0), stop=(c == 31)),
                                 reads=[r_hT[th * 4 + (k4)][c // 8] for k4 in range(4)] + [r_wb[wbi][c // 8]],
                                 writes=[r_psA[b]])
                        return b

                    def next_ob():
                        o = cnt["ob"] % 4
                        cnt["ob"] += 1
                        return o

                    blk_i = [0]

                    def vb_block(k, wbi):
                        for tt in range(NT):
                            b = tm_matmul(wbi, 512, tt)
                            o = next_ob()
                            copy_op(evac_eng(), ob[o][:], psA[b][:], [r_psA[b]], [r_ob[o]])
                            dst = AP(vbS, (win0 + tt * 128) * 2048 + k * 512, [[2048, 128], [1, 512]])
                            P.op("sp", lambda e, o=o, dst=dst: e.dma_start(out=dst, in_=ob[o][:]), reads=[r_ob[o]], dma=True)

                    def fm_block(k, wbi, dst_t, dst_cols, col_off, silu):
                        for cc in range(4):
                            for th in range(2):
                                b = fm_matmul(wbi, cc, th)
                                o = next_ob()
                                if silu:
                                    P.op("act", lambda e, o=o, b=b: e.activation(out=ob[o][:], in_=psA[b][:], func=AF.Silu),
                                         reads=[r_psA[b]], writes=[r_ob[o]])
                                else:
                                    copy_op(evac_eng(), ob[o][:], psA[b][:], [r_psA[b]], [r_ob[o]])
                                dst = AP(dst_t, (k * 512 + cc * 128) * dst_cols + col_off + th * 512, [[dst_cols, 128], [1, 512]])
                                P.op("sp", lambda e, o=o, dst=dst: e.dma_start(out=dst, in_=ob[o][:]), reads=[r_ob[o]], dma=True)

                    blocks = []
                    for k in range(4):
                        blocks.append((C_VB + k * 512, lambda wbi, k=k: vb_block(k, wbi)))
                    for k in range(4):
                        blocks.append((C_KB + k * 512, lambda wbi, k=k: fm_block(k, wbi, kbT, 3 * T, win0, False)))
                    if own:
                        for k in range(4):
                            blocks.append((C_QB + k * 512, lambda wbi, k=k: fm_block(k, wbi, qbT, T, 0, False)))
                        for k in range(4):
                            blocks.append((C_GA + k * 512, lambda wbi, k=k: fm_block(k, wbi, sgT, T, 0, True)))
                        for k in range(4):
                            blocks.append((C_GB + k * 512, lambda wbi, k=k: fm_block(4 + k, wbi, sgT, T, 0, True)))

                    if own:
                        with ExitStack() as s3:
                            gq = sbt(s3, "gq", [128, 1024], F32)
                            gkv = sbt(s3, "gkv", [128, 512], F32)
                            ctm = [sbt(s3, f"ctm{i}", [128, 1024], F32) for i in range(2)]
                            cbf = [sbt(s3, f"cbf{i}", [128, 1024], BF16) for i in range(2)]
                            junk2 = sbt(s3, "junkL", [128, 1024], BF16)
                            stat2 = [sbt(s3, f"stat2_{i}", [128, 4], F32) for i in range(2)]
                            psT = [pst(s3, f"psTl{i}", [128, 8, 128], BF16) for i in range(2)]
                            sin8 = sbt(s3, "sinK", [128, NT, 32], F32)
                            cos8 = sbt(s3, "cosK", [128, NT, 32], F32)
                            krt = [sbt(s3, f"krt{i}", [128, 64], F32) for i in range(2)]
                            kra = sbt(s3, "kra", [128, 4, 32], F32)
                            krb = [sbt(s3, f"krb{i}", [128, 64], BF16) for i in range(2)]
                            kpo = [sbt(s3, f"kpo{i}", [64, 128], BF16) for i in range(2)]
                            r_gq, r_gkv, r_junk2 = Reg(), Reg(), Reg()
                            r_ctm = [Reg(), Reg()]
                            r_cbf = [Reg(), Reg()]
                            r_stat2 = [Reg(), Reg()]
                            r_psT = [Reg(), Reg()]
                            r_tab = [Reg() for _ in range(NT)]
                            r_krt = [Reg(), Reg()]
                            r_kra = Reg()
                            r_krb = [Reg(), Reg()]
                            r_kpo = [Reg(), Reg()]
                            rr = [0]
                            P.op("sp", lambda e: e.dma_start(out=gq[:], in_=bcast(g_ql, 1024)), writes=[r_gq], dma=True)
                            P.op("sp", lambda e: e.dma_start(out=gkv[:], in_=bcast(g_kvl, 512)), writes=[r_gkv], dma=True)
                            rope_tables(s3, 32, sin8, cos8, r_tab, posf, r_posf, "k")

                            w_block_load(wb[0], r_wb[0], C_CQ, 512)
                            w_block_load(wb[1], r_wb[1], C_CQ + 512, 512)
                            for tt in range(NT):
                                cb = tt % 2
                                for k in range(2):
                                    b = tm_matmul(k, 512, tt)
                                    copy_op(evac_eng(), ctm[cb][:, k * 512:(k + 1) * 512], psA[b][:], [r_psA[b]], [r_ctm[cb]],
                                            acc=(k == 1))
                                rms_tm(ctm[cb][:], 1024, gq[:], cbf[cb][:], junk2[:], stat2[cb], r_stat2[cb], r_ctm[cb], r_gq,
                                       r_cbf[cb], r_junk2)
                                transpose_to(cbf[cb], 8, 128, psT, r_psT, rr, r_cbf[cb],
                                             lambda c0, n, tt=tt: cqnT[:, c0:c0 + n, tt * 128:(tt + 1) * 128],
                                             lambda c0, n, tt=tt: [r_cqnT[tt]])
                            w_block_load(wb[0], r_wb[0], C_CKV, 512)
                            w_block_load(wb[1], r_wb[1], C_KR, 64)
                            for tt in range(NT):
                                cb = tt % 2
                                b = tm_matmul(0, 512, tt)
                                copy_op(evac_eng(), ctm[cb][:, 0:512], psA[b][:], [r_psA[b]], [r_ctm[cb]])
                                rms_tm(ctm[cb][:, 0:512], 512, gkv[:], cbf[cb][:, 0:512], junk2[:, 0:512], stat2[cb], r_stat2[cb],
                                       r_ctm[cb], r_gkv, r_cbf[cb], r_junk2)
                                transpose_to(cbf[cb], 4, 128, psT, r_psT, rr, r_cbf[cb],
                                             lambda c0, n, tt=tt: ckvnT[:, c0:c0 + n, tt * 128:(tt + 1) * 128],
                                             lambda c0, n, tt=tt: [r_ckvnT[tt]])
                                b = tm_matmul(1, 64, tt)
                                copy_op("act", krt[cb][:], psA[b][:, 0:64], [r_psA[b]], [r_krt[cb]])
                                t1 = krt[cb][:, 0:32]
                                t2 = krt[cb][:, 32:64]
                                cs = cos8[:, tt, :]
                                sn = sin8[:, tt, :]
                                for i, (u, v) in enumerate(((t1, cs), (t2, sn), (t1, sn), (t2, cs))):
                                    P.op("dve", lambda e, i=i, u=u, v=v: e.tensor_tensor(out=kra[:, i, :], in0=u, in1=v, op=ALU.mult),
                                         reads=[r_krt[cb], r_tab[tt]], writes=[r_kra], acc=(i > 0))
                                P.op("dve", lambda e, cb=cb: e.tensor_tensor(out=krb[cb][:, 0:32], in0=kra[:, 0, :], in1=kra[:, 1, :],
                                                                             op=ALU.subtract), reads=[r_kra], writes=[r_krb[cb]])
                                P.op("dve", lambda e, cb=cb: e.tensor_tensor(out=krb[cb][:, 32:64], in0=kra[:, 2, :], in1=kra[:, 3, :],
                                                                             op=ALU.add), reads=[r_kra], writes=[r_krb[cb]], acc=True)
                                pb = rr[0] % 2
                                rr[0] += 1
                                P.op("pe", lambda e, cb=cb, pb=pb: e.transpose(psT[pb][0:64, 0, :], krb[cb][:, :], ident[:]),
                                     reads=[r_krb[cb], r_ident], writes=[r_psT[pb]])
                                copy_op("act", kpo[cb][:, :], psT[pb][0:64, 0, :], [r_psT[pb]], [r_kpo[cb]])
                                dst = AP(X, KPE_ROW * 1024 + tt * 128, [[1024, 64], [1, 128]])
                                P.op("sp", lambda e, cb=cb, dst=dst: e.dma_start(out=dst, in_=kpo[cb][:, :]), reads=[r_kpo[cb]], dma=True)
                            P.barrier()
                            P.emit()

                    if own:
                        with ExitStack() as skv:
                            wkvh = sbt(skv, "wkvh", [128, 4, 2048], BF16)
                            obk = [sbt(skv, f"obk{i}", [128, 512], BF16) for i in range(4)]
                            r_wkvh = Reg()
                            r_obk = [Reg() for _ in range(4)]
                            r_X = Reg("X")
                            r_G = Reg("G")
                            okc = [0]
                            src = AP(w_ukv, 0, [[4096, 128], [128 * 4096, 4], [1, 2048]])
                            P.op("pool", lambda e, src=src: e.dma_start(out=wkvh[:], in_=src), writes=[r_wkvh], dma=True)
                            for h in range(16):
                                for th in range(2):
                                    b = cnt["ps"] % 3
                                    cnt["ps"] += 1
                                    for c in range(4):
                                        P.op("pe", lambda e, c=c, b=b, h=h, th=th: e.matmul(
                                            psA[b][:], lhsT=wkvh[:, c, h * 128:(h + 1) * 128], rhs=ckvnT[:, c, th * 512:(th + 1) * 512],
                                            start=(c == 0), stop=(c == 3)),
                                            reads=[r_wkvh] + [r_ckvnT[th * 4 + k4] for k4 in range(4)], writes=[r_psA[b]])
                                    o = okc[0] % 4
                                    okc[0] += 1
                                    copy_op(evac_eng(), obk[o][:], psA[b][:], [r_psA[b]], [r_obk[o]])
                                    dst = AP(X, h * 128 * 1024 + th * 512, [[1024, 128], [1, 512]])
                                    P.op("sp", lambda e, o=o, dst=dst: e.dma_start(out=dst, in_=obk[o][:]), reads=[r_obk[o]],
                                         writes=[r_X], dma=True, acc=True)
                            src = AP(w_ukv, 2048, [[4096, 128], [128 * 4096, 4], [1, 2048]])
                            P.op("pool", lambda e, src=src: e.dma_start(out=wkvh[:], in_=src), writes=[r_wkvh], dma=True)
                            for tt in range(NT):
                                for k in range(4):
                                    b = cnt["ps"] % 3
                                    cnt["ps"] += 1
                                    for c in range(4):
                                        P.op("pe", lambda e, c=c, b=b, k=k, tt=tt: e.matmul(
                                            psA[b][:], lhsT=ckvnT[:, c, tt * 128:(tt + 1) * 128],
                                            rhs=wkvh[:, c, k * 512:(k + 1) * 512], start=(c == 0), stop=(c == 3)),
                                            reads=[r_wkvh, r_ckvnT[tt]], writes=[r_psA[b]])
                                    o = okc[0] % 4
                                    okc[0] += 1
                                    copy_op(evac_eng(), obk[o][:], psA[b][:], [r_psA[b]], [r_obk[o]])
                                    dst = AP(X, VA_OFF + (tt * 128) * 2048 + k * 512, [[2048, 128], [1, 512]])
                                    P.op("sp", lambda e, o=o, dst=dst: e.dma_start(out=dst, in_=obk[o][:]), reads=[r_obk[o]],
                                         writes=[r_X], dma=True, acc=True)
                            P.op("pool", lambda e: e.collective_compute("AllGather", ALU.bypass, replica_groups=[list(range(NCORES))],
                                                                        ins=[X.ap()], outs=[G.ap()]),
                                 reads=[r_X, r_kpe_all], writes=[r_G], own_sem=ccsem)
                            P.barrier()
                            P.emit()

                    ob = [sbt(s2, f"ob{pass_id}_{i}", [128, 512], BF16) for i in range(4)]
                    r_ob = [Reg() for _ in range(4)]
                    if blocks:
                        w_block_load(wb[0], r_wb[0], blocks[0][0], 512)
                    for i, (c0, fn) in enumerate(blocks):
                        if i + 1 < len(blocks):
                            w_block_load(wb[(i + 1) % 2], r_wb[(i + 1) % 2], blocks[i + 1][0], 512)
                        fn(i % 2)
                    P.barrier()
                    P.emit()

                sh.close()
                if own:
                    with ExitStack() as s4:
                        sin8 = sbt(s4, "sin8", [128, NT, 256], F32)
                        cos8 = sbt(s4, "cos8", [128, NT, 256], F32)
                        r_tab = [Reg() for _ in range(NT)]
                        rope_tables(s4, 256, sin8, cos8, r_tab, posf, r_posf, "q")
                        wqn = sbt(s4, "wqn", [128, 8, 2048], BF16)
                        wqp = sbt(s4, "wqp", [128, 8, 1024], BF16)
                        psA = [pst(s4, f"psC{i}", [128, 512], F32) for i in range(3)]
                        psT = [pst(s4, f"psTc{i}", [128, 8, 128], BF16) for i in range(2)]
                        ob = [sbt(s4, f"obc{i}", [128, 512], BF16) for i in range(4)]
                        qpf = [sbt(s4, f"qpf{i}", [128, 8, 64], F32) for i in range(2)]
                        qra = sbt(s4, "qra", [128, 4, 8, 32], F32)
                        qpb = [sbt(s4, f"qpb{i}", [128, 8, 64], BF16) for i in range(2)]
                        qpo = [sbt(s4, f"qpo{i}", [64, 8, 128], BF16) for i in range(2)]
                        r_wqn = [Reg() for _ in range(2)]
                        r_wqp = Reg()
                        r_psA = [Reg() for _ in range(3)]
                        r_psT = [Reg(), Reg()]
                        r_ob = [Reg() for _ in range(4)]
                        r_qpf = [Reg(), Reg()]
                        r_qra = Reg()
                        r_qpb = [Reg(), Reg()]
                        r_qpo = [Reg(), Reg()]
                        cnt = {"ps": 0, "ob": 0}
                        for g in range(2):
                            src = AP(w_uq, g * 1024, [[3072, 128], [128 * 3072, 8], [1, 1024]])
                            P.op("pool", lambda e, g=g, src=src: e.dma_start(out=wqn[:, :, g * 1024:(g + 1) * 1024], in_=src),
                                 writes=[r_wqn[g]], dma=True)
                        src = AP(w_uq, 2048, [[3072, 128], [128 * 3072, 8], [1, 1024]])
                        P.op("pool", lambda e, src=src: e.dma_start(out=wqp[:], in_=src), writes=[r_wqp], dma=True)

                        def nps():
                            b = cnt["ps"] % 3
                            cnt["ps"] += 1
                            return b

                        def nob():
                            o = cnt["ob"] % 4
                            cnt["ob"] += 1
                            return o

                        for h in range(16):
                            for th in range(2):
                                b = nps()
                                for c in range(8):
                                    P.op("pe", lambda e, c=c, b=b, h=h, th=th: e.matmul(
                                        psA[b][:], lhsT=wqn[:, c, h * 128:(h + 1) * 128], rhs=cqnT[:, c, th * 512:(th + 1) * 512],
                                        start=(c == 0), stop=(c == 7)),
                                        reads=[r_wqn[h // 8]] + [r_cqnT[th * 4 + k4] for k4 in range(4)], writes=[r_psA[b]])
                                o = nob()
                                copy_op(evac_eng(), ob[o][:], psA[b][:], [r_psA[b]], [r_ob[o]])
                                dst = AP(QnT, h * 128 * T + th * 512, [[T, 128], [1, 512]])
                                P.op("sp", lambda e, o=o, dst=dst: e.dma_start(out=dst, in_=ob[o][:]), reads=[r_ob[o]], dma=True)
                        rr = [0]
                        for tt in range(NT):
                            for k in range(2):
                                cb = (tt * 2 + k) % 2
                                b = nps()
                                for c in range(8):
                                    P.op("pe", lambda e, c=c, b=b, k=k, tt=tt: e.matmul(
                                        psA[b][:], lhsT=cqnT[:, c, tt * 128:(tt + 1) * 128], rhs=wqp[:, c, k * 512:(k + 1) * 512],
                                        start=(c == 0), stop=(c == 7)),
                                        reads=[r_wqp, r_cqnT[tt]], writes=[r_psA[b]])
                                P.op("act", lambda e, b=b, cb=cb: e.activation(out=qpf[cb][:].rearrange("p a b -> p (a b)"),
                                                                               in_=psA[b][:], func=AF.Copy),
                                     reads=[r_psA[b]], writes=[r_qpf[cb]])
                                t1 = qpf[cb][:, :, 0:32]
                                t2 = qpf[cb][:, :, 32:64]
                                cs = cos8[:, tt, :].rearrange("p (a b) -> p a b", b=32)
                                sn = sin8[:, tt, :].rearrange("p (a b) -> p a b", b=32)
                                for i, (u, v) in enumerate(((t1, cs), (t2, sn), (t1, sn), (t2, cs))):
                                    P.op("dve", lambda e, i=i, u=u, v=v: e.tensor_tensor(out=qra[:, i, :, :], in0=u, in1=v, op=ALU.mult),
                                         reads=[r_qpf[cb], r_tab[tt]], writes=[r_qra], acc=(i > 0))
                                P.op("dve", lambda e, cb=cb: e.tensor_tensor(out=qpb[cb][:, :, 0:32], in0=qra[:, 0, :, :],
                                                                             in1=qra[:, 1, :, :], op=ALU.subtract),
                                     reads=[r_qra], writes=[r_qpb[cb]])
                                P.op("dve", lambda e, cb=cb: e.tensor_tensor(out=qpb[cb][:, :, 32:64], in0=qra[:, 2, :, :],
                                                                             in1=qra[:, 3, :, :], op=ALU.add),
                                     reads=[r_qra], writes=[r_qpb[cb]], acc=True)
                                pb = rr[0] % 2
                                rr[0] += 1
                                for hh in range(8):
                                    P.op("pe", lambda e, cb=cb, pb=pb, hh=hh: e.transpose(psT[pb][0:64, hh, :], qpb[cb][:, hh, :], ident[:]),
                                         reads=[r_qpb[cb], r_ident], writes=[r_psT[pb]])
                                copy_op(evac_eng(), qpo[cb][:, :, :], psT[pb][0:64, :, :], [r_psT[pb]], [r_qpo[cb]])
                                dst = AP(QpeT, (k * 8) * 64 * T + tt * 128, [[T, 64], [64 * T, 8], [1, 128]])
                                P.op("sp", lambda e, cb=cb, dst=dst: e.dma_start(out=dst, in_=qpo[cb][:, :, :]), reads=[r_qpo[cb]], dma=True)
                        P.barrier()
                        P.emit()
                    return r_G
            return None

        r_kpe_all = Reg("kpe_all")
        projection_pass(1)
        projection_pass(2)
        r_G = projection_pass(0)
        if "dbgG" in debug:
            dbgG = nc.dram_tensor("dbgG", [NCORES, 320, 1024], BF16, kind="ExternalOutput")
            with ExitStack() as sd:
                tmpg = sbt(sd, "tmpg", [128, 8, 1024], BF16)
                r_tmpg = Reg()
                for r in range(NCORES):
                    for (row0, nrow, drow) in ((KPE_ROW, 64, 0), (5 * 128, 128, 64), (2112 + 256, 128, 192)):
                        src = AP(G, (r * ROWS + row0) * 1024, [[1024, nrow], [1, 1024]])
                        dst = AP(dbgG, (r * 320 + drow) * 1024, [[1024, nrow], [1, 1024]])
                        P.op("sp", lambda e, src=src, nrow=nrow: e.dma_start(out=tmpg[0:nrow, 0, :], in_=src), reads=[r_G],
                             writes=[r_tmpg], dma=True)
                        P.op("sp", lambda e, dst=dst, nrow=nrow: e.dma_start(out=dst, in_=tmpg[0:nrow, 0, :]), reads=[r_tmpg], dma=True)
                P.barrier()
                P.emit()
        sy = ExitStack()
        yT = sbt(sy, "yT", [128, 32, T], BF16)
        ssq = sbt(sy, "ssq", [128, 2, T], F32)
        for a in range(2):
            for hq in range(2):
                P.op("pool", lambda e, a=a, hq=hq: e.memset(ssq[:, a, hq * 512:(hq + 1) * 512], 0.0),
                     writes=[r_ssq[a][hq]])
        with ExitStack() as s5:
            rinv = sbt(s5, "rinv", [128, 512], F32)
            ytmp = sbt(s5, "ytmp", [128, 512], F32)
            ysq = sbt(s5, "ysq", [128, 512], F32)
            r_rinv, r_ytmp, r_ysq = Reg(), Reg(), Reg()
            cnt = {"s": 0, "p": 0, "o": 0}

            def finalize(psO_, psL_, r_psO_, r_psL_, a, chunk, qc):
                P.op("dve", lambda e: e.reciprocal(out=rinv[:], in_=psL_[:]), reads=[r_psL_], writes=[r_rinv])
                P.op("dve", lambda e: e.tensor_tensor(out=ytmp[:], in0=psO_[:], in1=rinv[:], op=ALU.mult),
                     reads=[r_psO_, r_rinv], writes=[r_ytmp])
                P.op("act", lambda e: e.activation(out=yT[:, chunk, qc * 512:(qc + 1) * 512], in_=ytmp[:], func=AF.Copy),
                     reads=[r_ytmp], writes=[r_yT[chunk][qc]])
                P.op("act", lambda e: e.activation(out=ysq[:], in_=ytmp[:], func=AF.Square), reads=[r_ytmp], writes=[r_ysq])
                P.op("pool", lambda e: e.tensor_tensor(out=ssq[:, a, qc * 512:(qc + 1) * 512], in0=ssq[:, a, qc * 512:(qc + 1) * 512],
                                                       in1=ysq[:], op=ALU.add), reads=[r_ysq, r_ssq[a][qc]], writes=[r_ssq[a][qc]])

            def run_pipeline(tiles, skew=2, nS=3, nP=4):
                n = len(tiles)
                for i in range(n + skew):
                    if i < n:
                        tiles[i][0](i % nS)
                        tiles[i][1](i % nS, i % nP)
                    j = i - skew
                    if j >= 0:
                        tiles[j][2](j % nP)
                        if tiles[j][3] is not None:
                            tiles[j][3]()

            def mla_phase():
              with ExitStack() as s6:
                kn = [sbt(s6, f"kn{i}", [128, S], BF16) for i in range(2)]
                vv = [sbt(s6, f"vv{i}", [128, 64, 128], BF16) for i in range(2)]
                kpe = sbt(s6, "kpe", [128, S], BF16)
                qn = [sbt(s6, f"qn{i}", [128, T], BF16) for i in range(2)]
                qp = [sbt(s6, f"qp{i}", [128, T], BF16) for i in range(2)]
                r_kz = Reg("kz")
                r_qp = [Reg(), Reg()]
                r_kpe = [Reg() for _ in range(8)]
                P.op("pool", lambda e: e.memset(kpe[:, :], 0.0), writes=r_kpe + [r_kz])
                for i in range(2):
                    P.op("pool", lambda e, i=i: e.memset(qp[i][:, :], 0.0), writes=[r_qp[i]])
                r_kn = [[Reg() for _ in range(8)] for _ in range(2)]
                r_vv = [[Reg() for _ in range(8)] for _ in range(2)]
                r_qn = [Reg(), Reg()]
                SC_A = 192.0 ** -0.5
                for r in range(8):
                    src = AP(G, (r * ROWS + KPE_ROW) * 1024, [[1024, 64], [1, 1024]])
                    P.op("sp", lambda e, r=r, src=src: e.dma_start(out=kpe[0:64, r * 1024:(r + 1) * 1024], in_=src),
                         reads=[r_G], writes=[r_kpe[r]], dma=True)

                def load_head(h):
                    hb = h % 2
                    for r in range(8):
                        src = AP(G, (r * ROWS + h * 128) * 1024, [[1024, 128], [1, 1024]])
                        P.op("sp", lambda e, r=r, src=src: e.dma_start(out=kn[hb][:, r * 1024:(r + 1) * 1024], in_=src),
                             reads=[r_G], writes=[r_kn[hb][r]], dma=True)
                        src = AP(G, r * ROWS * 1024 + VA_OFF + h * 128, [[2048, 128], [128 * 2048, 8], [1, 128]])
                        P.op("sp", lambda e, r=r, src=src: e.dma_start(out=vv[hb][:, r * 8:(r + 1) * 8, :], in_=src),
                             reads=[r_G], writes=[r_vv[hb][r]], dma=True)
                    src = AP(QnT, h * 128 * T, [[T, 128], [1, T]])
                    P.op("sp", lambda e, src=src: e.dma_start(out=qn[hb][:], in_=src), writes=[r_qn[hb]], dma=True)
                    src = AP(QpeT, h * 64 * T, [[T, 64], [1, T]])
                    P.op("sp", lambda e, src=src: e.dma_start(out=qp[hb][0:64, :], in_=src), writes=[r_qp[hb]], dma=True)

                psSS = [pst(s6, f"psSS{i}", [128, 1024], F32) for i in range(3)]
                psO = [pst(s6, "psOa", [128, 512], F32)]
                psL = [pst(s6, "psLa", [128, 512], F32)]
                r_psSS = [Reg() for _ in range(3)]
                r_psO = [Reg()]
                r_psL = [Reg()]
                pbuf2 = [sbt(s6, f"pbuf2_{i}", [128, 1024], BF16) for i in range(3)]
                accb = [sbt(s6, f"accb{i}", [128, 1024], F32) for i in range(2)]
                r_accb = [[Reg(), Reg()], [Reg(), Reg()]]
                r_pbuf2 = [Reg() for _ in range(3)]
                load_head(0)
                load_head(1)
                tiles = []
                for h in range(16):
                    hb = h % 2
                    for qc in range(2):
                        for pp in range(32):
                            def qk(sb_, hb=hb, pp=pp, qc=qc):
                                for half in range(2):
                                    kb = 2 * pp + half
                                    r8 = kb // 8
                                    P.op("pe", lambda e, kb=kb, half=half: e.matmul(
                                        psSS[sb_][:, half * 512:(half + 1) * 512], lhsT=kn[hb][:, kb * 128:(kb + 1) * 128],
                                        rhs=qn[hb][:, qc * 512:(qc + 1) * 512], start=True, stop=False),
                                        reads=[r_kn[hb][r8], r_qn[hb]], writes=[r_psSS[sb_]])
                                    P.op("pe", lambda e, kb=kb, half=half: e.matmul(
                                        psSS[sb_][:, half * 512:(half + 1) * 512], lhsT=kpe[:, kb * 128:(kb + 1) * 128],
                                        rhs=qp[hb][:, qc * 512:(qc + 1) * 512], start=False, stop=True),
                                        reads=[r_kpe[r8], r_qp[hb], r_kz], writes=[r_psSS[sb_]])

                            def mid(sb_, pb_, pp=pp, ab=(h * 2 + qc) % 2):
                                P.op("act", lambda e: e.activation(out=pbuf2[pb_][:], in_=psSS[sb_][:], func=AF.Exp, scale=SC_A),
                                     reads=[r_psSS[sb_]], writes=[r_pbuf2[pb_]])
                                for half, eng_ in ((0, "dve"), (1, "pool")):
                                    src = pbuf2[pb_][:, half * 512:(half + 1) * 512]
                                    dst_ = accb[ab][:, half * 512:(half + 1) * 512]
                                    if pp == 0:
                                        P.op(eng_, lambda e, src=src, dst_=dst_: e.tensor_copy(out=dst_, in_=src),
                                             reads=[r_pbuf2[pb_]], writes=[r_accb[ab][half]])
                                    else:
                                        P.op(eng_, lambda e, src=src, dst_=dst_: e.tensor_tensor(out=dst_, in0=dst_, in1=src, op=ALU.add),
                                             reads=[r_pbuf2[pb_], r_accb[ab][half]], writes=[r_accb[ab][half]])

                            def pv(pb_, hb=hb, pp=pp):
                                for half in range(2):
                                    kb = 2 * pp + half
                                    r8 = kb // 8
                                    P.op("pe", lambda e, kb=kb, half=half: e.matmul(
                                        psO[0][:], lhsT=vv[hb][:, kb, :], rhs=pbuf2[pb_][:, half * 512:(half + 1) * 512],
                                        start=(kb == 0), stop=(kb == 63)),
                                        reads=[r_vv[hb][r8], r_pbuf2[pb_]], writes=[r_psO[0]])

                            after = None
                            if pp == 31:
                                def after(h=h, qc=qc, ab=(h * 2 + qc) % 2):
                                    for half in range(2):
                                        P.op("pe", lambda e, half=half: e.matmul(psL[0][:], lhsT=ones_f[:],
                                                                                 rhs=accb[ab][:, half * 512:(half + 1) * 512],
                                                                                 start=(half == 0), stop=(half == 1)),
                                             reads=[r_onesf, r_accb[ab][half]], writes=[r_psL[0]])
                                    finalize(psO[0], psL[0], r_psO[0], r_psL[0], 0, h, qc)
                                    if qc == 1 and h + 2 < 16:
                                        load_head(h + 2)
                            tiles.append((qk, mid, pv, after))
                run_pipeline(tiles, skew=2, nS=3, nP=3)
                P.barrier()
                P.emit()

            def dil_phase():
              with ExitStack() as s7:
                psS = [pst(s7, f"psS{i}", [128, 512], F32) for i in range(4)]
                psO = [pst(s7, f"psO{i}", [128, 512], F32) for i in range(2)]
                psL = [pst(s7, f"psL{i}", [128, 512], F32) for i in range(2)]
                r_psS = [Reg() for _ in range(4)]
                r_psO = [Reg(), Reg()]
                r_psL = [Reg(), Reg()]
                pbuf = [sbt(s7, f"pbuf{i}", [128, 512], BF16) for i in range(4)]
                r_pbuf = [Reg() for _ in range(4)]
                tabs = sbt(s7, "tabs", [128, TABW], F32)
                tln = sbt(s7, "tln", [128, TABW], F32)
                bh = [sbt(s7, f"bh{i}", [128, TABW], F32) for i in range(2)]
                kw = [sbt(s7, f"kw{i}", [128, 3 * T], BF16) for i in range(2)]
                vw = [sbt(s7, f"vw{i}", [128, 24, 128], BF16) for i in range(2)]
                qb = [sbt(s7, f"qbq{i}", [128, T], BF16) for i in range(2)]
                tb = [sbt(s7, f"tb{i}", [128, 512], F32) for i in range(4)]
                r_tabs, r_tln = Reg(), Reg()
                r_bh = [Reg(), Reg()]
                r_kw = [Reg(), Reg()]
                r_vw = [Reg(), Reg()]
                r_qb = [Reg(), Reg()]
                r_tb = [Reg() for _ in range(4)]
                SC_B = 128.0 ** -0.5
                tcnt = [0]
                P.op("sp", lambda e: e.dma_start(out=tabs[:], in_=c_tabs.ap()), writes=[r_tabs], dma=True)
                P.op("sp", lambda e: e.dma_start(out=tln[:], in_=c_tln.ap()), writes=[r_tln], dma=True)

                def load_head_b(h):
                    hb = h % 2
                    src = AP(kbT, h * 128 * 3 * T, [[3 * T, 128], [1, 3 * T]])
                    P.op("sp", lambda e, src=src: e.dma_start(out=kw[hb][:], in_=src), writes=[r_kw[hb]], dma=True)
                    src = AP(vbS, h * 128, [[2048, 128], [128 * 2048, 24], [1, 128]])
                    P.op("sp", lambda e, src=src: e.dma_start(out=vw[hb][:], in_=src), writes=[r_vw[hb]], dma=True)
                    src = AP(qbT, h * 128 * T, [[T, 128], [1, T]])
                    P.op("sp", lambda e, src=src: e.dma_start(out=qb[hb][:], in_=src), writes=[r_qb[hb]], dma=True)
                    P.op("dve", lambda e: e.scalar_tensor_tensor(out=bh[hb][:], in0=tabs[:], scalar=nslope[:, h:h + 1], in1=tln[:],
                                                                  op0=ALU.mult, op1=ALU.add),
                         reads=[r_tabs, r_tln, r_nsl], writes=[r_bh[hb]])

                load_head_b(0)
                load_head_b(1)
                tiles = []
                tcnt = [0]
                for h in range(16):
                    hb = h % 2
                    for qc in range(2):
                        ob_ = (h * 2 + qc) % 2
                        for j in range(20):
                            kbw = 4 * qc + j
                            ecol = 0 if kbw < 8 else (1 if kbw >= 16 else 2)
                            off = (19 - j) * 128
                            t_ = tcnt[0] % 4
                            tcnt[0] += 1

                            def qk(sb_, hb=hb, kbw=kbw, qc=qc):
                                P.op("pe", lambda e: e.matmul(
                                    psS[sb_][:], lhsT=kw[hb][:, kbw * 128:(kbw + 1) * 128], rhs=qb[hb][:, qc * 512:(qc + 1) * 512],
                                    start=True, stop=True), reads=[r_kw[hb], r_qb[hb]], writes=[r_psS[sb_]])

                            def mid(sb_, pb_, hb=hb, off=off, t_=t_, ecol=ecol):
                                P.op("dve", lambda e: e.scalar_tensor_tensor(
                                    out=tb[t_][:], in0=psS[sb_][:], scalar=SC_B, in1=bh[hb][:, off:off + 512], op0=ALU.mult, op1=ALU.add),
                                    reads=[r_psS[sb_], r_bh[hb]], writes=[r_tb[t_]])
                                P.op("act", lambda e: e.activation(
                                    out=pbuf[pb_][:], in_=tb[t_][:], func=AF.Exp, bias=edge[:, ecol:ecol + 1]),
                                    reads=[r_tb[t_], r_edge], writes=[r_pbuf[pb_]])

                            def pv(pb_, hb=hb, kbw=kbw, j=j, ob_=ob_):
                                P.op("pe", lambda e: e.matmul(
                                    psO[ob_][:], lhsT=vw[hb][:, kbw, :], rhs=pbuf[pb_][:], start=(j == 0), stop=(j == 19)),
                                    reads=[r_vw[hb], r_pbuf[pb_]], writes=[r_psO[ob_]])
                                P.op("pe", lambda e: e.matmul(
                                    psL[ob_][:], lhsT=ones_bf[:], rhs=pbuf[pb_][:], start=(j == 0), stop=(j == 19)),
                                    reads=[r_ones, r_pbuf[pb_]], writes=[r_psL[ob_]])

                            after = None
                            if j == 19:
                                def after(h=h, qc=qc, ob_=ob_):
                                    finalize(psO[ob_], psL[ob_], r_psO[ob_], r_psL[ob_], 1, 16 + h, qc)
                                    if qc == 1 and h + 2 < 16:
                                        load_head_b(h + 2)
                            tiles.append((qk, mid, pv, after))
                run_pipeline(tiles, skew=3, nS=4, nP=4)
                P.barrier()
                P.emit()

            dil_phase()
            mla_phase()

        with ExitStack() as s8:
            zT = yT
            r_zT = [Reg() for _ in range(32)]
            psA = [pst(s8, f"psA3_{i}", [128, 512], F32) for i in range(3)]
            r_psA = [Reg() for _ in range(3)]
            psB = [pst(s8, f"psB3_{i}", [128, 512], F32) for i in range(2)]
            r_psB = [Reg(), Reg()]

            def w3_load(wt, wbuf, r_wbuf, col0):
                for g in range(4):
                    src = AP(wt, g * 8 * 128 * D + col0, [[D, 128], [128 * D, 8], [1, 512]])
                    P.op("pool", lambda e, g=g, src=src: e.dma_start(out=wbuf[:, g * 8:(g + 1) * 8, :], in_=src),
                         writes=[r_wbuf[g]], dma=True)

            with ExitStack() as s9:
                rs = sbt(s9, "rs", [128, 2, T], F32)
                r_rs = [[Reg(), Reg()], [Reg(), Reg()]]
                sg = [sbt(s9, f"sg{i}", [128, T], BF16) for i in range(3)]
                r_sg = [Reg() for _ in range(3)]
                zt = [sbt(s9, f"zt{i}", [128, T], F32) for i in range(2)]
                r_zt = [Reg(), Reg()]
                for a in range(2):
                    for hq in range(2):
                        b = (a * 2 + hq) % 3
                        P.op("pe", lambda e, a=a, hq=hq, b=b: e.matmul(psA[b][:], lhsT=ones_f[:], rhs=ssq[:, a, hq * 512:(hq + 1) * 512],
                                                                       start=True, stop=True),
                             reads=[r_onesf, r_ssq[a][hq]], writes=[r_psA[b]])
                        P.op("dve", lambda e, a=a, hq=hq, b=b: e.tensor_scalar(out=rs[:, a, hq * 512:(hq + 1) * 512], in0=psA[b][:],
                                                                               scalar1=1.0 / 2048, scalar2=EPS, op0=ALU.mult, op1=ALU.add),
                             reads=[r_psA[b]], writes=[r_rs[a][hq]])
                        P.op("act", lambda e, a=a, hq=hq: e.sqrt(out=rs[:, a, hq * 512:(hq + 1) * 512], in_=rs[:, a, hq * 512:(hq + 1) * 512]),
                             reads=[r_rs[a][hq]], writes=[r_rs[a][hq]])
                        P.op("dve", lambda e, a=a, hq=hq: e.reciprocal(out=rs[:, a, hq * 512:(hq + 1) * 512],
                                                                       in_=rs[:, a, hq * 512:(hq + 1) * 512]),
                             reads=[r_rs[a][hq]], writes=[r_rs[a][hq]])
                for c in range(32):
                    a = c // 16
                    gt = goa if a == 0 else gob
                    r_gt = r_goa if a == 0 else r_gob
                    sb_ = c % 3
                    zb = c % 2
                    src = AP(sgT, c * 128 * T, [[T, 128], [1, T]])
                    P.op("sp", lambda e, sb_=sb_, src=src: e.dma_start(out=sg[sb_][:], in_=src), writes=[r_sg[sb_]], dma=True)
                    eng = "dve" if c % 2 == 0 else "pool"
                    P.op("dve", lambda e, c=c, a=a, gt=gt, zb=zb: e.scalar_tensor_tensor(
                        out=zt[zb][:], in0=yT[:, c, :], scalar=gt[:, (c % 16):(c % 16) + 1], in1=rs[:, a, :], op0=ALU.mult, op1=ALU.mult),
                        reads=[r_yT[c][0], r_yT[c][1], r_gt, r_rs[a][0], r_rs[a][1]], writes=[r_zt[zb]])
                    P.op(eng, lambda e, c=c, zb=zb, sb_=sb_: e.tensor_tensor(out=zT[:, c, :], in0=zt[zb][:], in1=sg[sb_][:], op=ALU.mult),
                         reads=[r_zt[zb], r_sg[sb_]], writes=[r_zT[c]])
                P.barrier()
                P.emit()

            with ExitStack() as s10:
                wb = [sbt(s10, f"wb3_{i}", [128, 32, 512], BF16) for i in range(2)]
                r_wb = [[Reg() for _ in range(4)] for _ in range(2)]
                w3_load(w_out, wb[0], r_wb[0], 0)
                xr = [sbt(s10, f"xr{i}", [128, 512], F32) for i in range(3)]
                r_xr = [Reg() for _ in range(3)]
                xo = [sbt(s10, f"xo{i}", [128, 512], F32) for i in range(3)]
                r_xo = [Reg() for _ in range(3)]
                n = 0
                for cb in range(8):
                    if cb + 1 < 8:
                        w3_load(w_out, wb[(cb + 1) % 2], r_wb[(cb + 1) % 2], (cb + 1) * 512)
                    wi = cb % 2
                    for tt in range(NT):
                        b = n % 3
                        n += 1
                        src = AP(x_ext, (T + tt * 128) * D + cb * 512, [[D, 128], [1, 512]])
                        P.op("sp", lambda e, b=b, src=src: e.dma_start(out=xr[b][:], in_=src), writes=[r_xr[b]], dma=True)
                        for c in range(32):
                            P.op("pe", lambda e, c=c, b=b, tt=tt, wi=wi: e.matmul(
                                psA[b][:], lhsT=zT[:, c, tt * 128:(tt + 1) * 128], rhs=wb[wi][:, c, :], start=(c == 0), stop=(c == 31)),
                                reads=[r_zT[c], r_wb[wi][c // 8]], writes=[r_psA[b]])
                        P.op("dve", lambda e, b=b: e.tensor_tensor(out=xo[b][:], in0=psA[b][:], in1=xr[b][:], op=ALU.add),
                             reads=[r_psA[b], r_xr[b]], writes=[r_xo[b]])
                        dst = AP(x1S, (tt * 128) * D + cb * 512, [[D, 128], [1, 512]])
                        P.op("sp", lambda e, b=b, dst=dst: e.dma_start(out=dst, in_=xo[b][:]), reads=[r_xo[b]], dma=True)
                P.barrier()
                P.emit()

            uT = zT
            r_uT = [[Reg() for _ in range(4)] for _ in range(NT)]
            pT = sbt(s8, "pT", [128, 2, T], BF16)
            r_pT = [Reg() for _ in range(NT)]
            with ExitStack() as s11:
                gbc = sbt(s11, "gbc3", [128, D], F32)
                xt = [sbt(s11, f"xt3_{i}", [128, D], F32) for i in range(2)]
                xn = [sbt(s11, f"xn3_{i}", [128, D], BF16) for i in range(2)]
                junk = sbt(s11, "junk3", [128, D], BF16)
                stat = [sbt(s11, f"stat3_{i}", [128, 4], F32) for i in range(2)]
                psT = [pst(s11, f"psT3_{i}", [128, 8, 128], BF16) for i in range(2)]
                pt = [sbt(s11, f"pt{i}", [128, 256], F32) for i in range(2)]
                ptb = [sbt(s11, f"ptb{i}", [128, 256], BF16) for i in range(2)]
                r_gbc, r_junk = Reg(), Reg()
                r_xt = [Reg(), Reg()]
                r_xn = [Reg(), Reg()]
                r_stat = [Reg(), Reg()]
                r_psT = [Reg(), Reg()]
                r_pt = [Reg(), Reg()]
                r_ptb = [Reg(), Reg()]
                rr = [0]
                P.op("sp", lambda e: e.dma_start(out=gbc[:], in_=bcast(g_ple, D)), writes=[r_gbc], dma=True)
                for tt in range(NT):
                    b = tt % 2
                    src = AP(x1S, (tt * 128) * D, [[D, 128], [1, D]])
                    P.op("sp", lambda e, b=b, src=src: e.dma_start(out=xt[b][:], in_=src), writes=[r_xt[b]], dma=True)
                    rms_tm(xt[b][:], D, gbc[:], xn[b][:], junk[:], stat[b], r_stat[b], r_xt[b], r_gbc, r_xn[b], r_junk)
                    transpose_to(xn[b], 32, 128, psT, r_psT, rr, r_xn[b],
                                 lambda c0, n, tt=tt: uT[:, c0:c0 + n, tt * 128:(tt + 1) * 128],
                                 lambda c0, n, tt=tt: [r_uT[tt][c0 // 8]])
                    src = AP(p_in, (tt * 128) * 256, [[256, 128], [1, 256]])
                    P.op("sp", lambda e, b=b, src=src: e.dma_start(out=pt[b][:], in_=src), writes=[r_pt[b]], dma=True)
                    P.op("pool", lambda e, b=b: e.tensor_copy(out=ptb[b][:], in_=pt[b][:]), reads=[r_pt[b]], writes=[r_ptb[b]])
                    transpose_to(ptb[b], 2, 128, psT, r_psT, rr, r_ptb[b],
                                 lambda c0, n, tt=tt: pT[:, c0:c0 + n, tt * 128:(tt + 1) * 128],
                                 lambda c0, n, tt=tt: [r_pT[tt]])
                P.barrier()
                P.emit()

            with ExitStack() as s12:
                wb = [sbt(s12, f"wb4_{i}", [128, 32, 512], BF16) for i in range(2)]
                r_wb = [[Reg() for _ in range(4)] for _ in range(2)]
                wpl = sbt(s12, "wpl", [128, 2, D], BF16)
                r_wpl = Reg()
                xr = [sbt(s12, f"xr4_{i}", [128, 512], F32) for i in range(3)]
                r_xr = [Reg() for _ in range(3)]
                sgm = [sbt(s12, f"sgm{i}", [128, 512], F32) for i in range(3)]
                r_sgm = [Reg() for _ in range(3)]
                xo = [sbt(s12, f"xo4_{i}", [128, 512], F32) for i in range(3)]
                r_xo = [Reg() for _ in range(3)]
                src = AP(w_ple, 0, [[D, 128], [128 * D, 2], [1, D]])
                P.op("pool", lambda e, src=src: e.dma_start(out=wpl[:], in_=src), writes=[r_wpl], dma=True)
                w3_load(w_pg, wb[0], r_wb[0], 0)
                n = 0
                for cb in range(8):
                    if cb + 1 < 8:
                        w3_load(w_pg, wb[(cb + 1) % 2], r_wb[(cb + 1) % 2], (cb + 1) * 512)
                    wi = cb % 2
                    for tt in range(NT):
                        b = n % 3
                        b2 = n % 2
                        n += 1
                        src = AP(x1S, (tt * 128) * D + cb * 512, [[D, 128], [1, 512]])
                        P.op("sp", lambda e, b=b, src=src: e.dma_start(out=xr[b][:], in_=src), writes=[r_xr[b]], dma=True)
                        for c in range(32):
                            P.op("pe", lambda e, c=c, b=b, tt=tt, wi=wi: e.matmul(
                                psA[b][:], lhsT=uT[:, c, tt * 128:(tt + 1) * 128], rhs=wb[wi][:, c, :], start=(c == 0), stop=(c == 31)),
                                reads=[r_uT[tt][c // 8], r_wb[wi][c // 8]], writes=[r_psA[b]])
                        for c in range(2):
                            P.op("pe", lambda e, c=c, b2=b2, tt=tt, cb=cb: e.matmul(
                                psB[b2][:], lhsT=pT[:, c, tt * 128:(tt + 1) * 128], rhs=wpl[:, c, cb * 512:(cb + 1) * 512],
                                start=(c == 0), stop=(c == 1)), reads=[r_pT[tt], r_wpl], writes=[r_psB[b2]])
                        P.op("act", lambda e, b=b: e.activation(out=sgm[b][:], in_=psA[b][:], func=AF.Sigmoid),
                             reads=[r_psA[b]], writes=[r_sgm[b]])
                        P.op("dve", lambda e, b=b, b2=b2: e.tensor_tensor(out=sgm[b][:], in0=psB[b2][:], in1=sgm[b][:], op=ALU.mult),
                             reads=[r_psB[b2], r_sgm[b]], writes=[r_sgm[b]])
                        P.op("pool", lambda e, b=b: e.tensor_tensor(out=xo[b][:], in0=sgm[b][:], in1=xr[b][:], op=ALU.add),
                             reads=[r_sgm[b], r_xr[b]], writes=[r_xo[b]])
                        dst = AP(x2S, (tt * 128) * D + cb * 512, [[D, 128], [1, 512]])
                        P.op("sp", lambda e, b=b, dst=dst: e.dma_start(out=dst, in_=xo[b][:]), reads=[r_xo[b]], dma=True)
                P.barrier()
                P.emit()

        sy.close()
        with ExitStack() as s13:
            gbc = sbt(s13, "gbc5", [128, D], F32)
            xt = [sbt(s13, f"xt5_{i}", [128, D], F32) for i in range(2)]
            xo = [sbt(s13, f"xo5_{i}", [128, D], F32) for i in range(2)]
            junk = sbt(s13, "junk5", [128, D], BF16)
            stat = [sbt(s13, f"stat5_{i}", [128, 4], F32) for i in range(2)]
            r_gbc, r_junk = Reg(), Reg()
            r_xt = [Reg(), Reg()]
            r_xo = [Reg(), Reg()]
            r_stat = [Reg(), Reg()]
            P.op("sp", lambda e: e.dma_start(out=gbc[:], in_=bcast(g_fin, D)), writes=[r_gbc], dma=True)
            for tt in range(NT):
                b = tt % 2
                src = AP(x2S, (tt * 128) * D, [[D, 128], [1, D]])
                P.op("sp", lambda e, b=b, src=src: e.dma_start(out=xt[b][:], in_=src), writes=[r_xt[b]], dma=True)
                rms_tm(xt[b][:], D, gbc[:], xo[b][:], junk[:], stat[b], r_stat[b], r_xt[b], r_gbc, r_xo[b], r_junk)
                dst = AP(out_t, (tt * 128) * D, [[D, 128], [1, D]])
                P.op("sp", lambda e, b=b, dst=dst: e.dma_start(out=dst, in_=xo[b][:]), reads=[r_xo[b]], dma=True)
            P.barrier()
            P.emit()
    except _Stop:
        pass
    return nc


_NC_CACHE = {}


def _mult(d):
    a = np.abs(d)
    return ((a <= 64).astype(np.float64) + ((d % 4 == 0) & (a <= 256)) + ((d % 16 == 0) & (a <= 1024)))


def _prep(x, p, positions, g_mix, w_in, g_q_latent, w_uq, g_kv_latent, w_ukv,
          g_out_mla, g_out_dil, w_out, w_ple, g_ple, w_ple_gate, g_final):
    f32 = np.float32
    x = np.asarray(x, f32).reshape(S, D)
    p = np.asarray(p, f32).reshape(S, 256)
    positions = np.asarray(positions).reshape(S).astype(np.int32)
    w_in = np.asarray(w_in, f32).reshape(D, IN_W)
    o = [0, 1024, 1536, 1600, 3648, 5696, 7744, 9792, 11840]
    cq, ckv, kr, ga, qb, kb, vb, gb = [w_in[:, o[i]:o[i + 1]] for i in range(8)]
    w_in_p = np.ascontiguousarray(np.concatenate([cq, ckv, kr, vb, ga, gb, qb, kb], axis=1))
    wq = np.asarray(w_uq, f32).reshape(1024, 16, 192)
    w_uq_p = np.ascontiguousarray(np.concatenate([wq[:, :, :128].reshape(1024, 2048), wq[:, :, 128:].reshape(1024, 1024)], axis=1))
    wk = np.asarray(w_ukv, f32).reshape(512, 16, 256)
    w_ukv_p = np.ascontiguousarray(np.concatenate([wk[:, :, :128].reshape(512, 2048), wk[:, :, 128:].reshape(512, 2048)], axis=1))
    w_out_ = np.ascontiguousarray(np.asarray(w_out, f32).reshape(D, D))
    w_ple_ = np.ascontiguousarray(np.asarray(w_ple, f32).reshape(256, D))
    w_pg_ = np.ascontiguousarray(np.asarray(w_ple_gate, f32).reshape(D, D))

    inv = (f32(10000.0) ** (-(np.arange(32, dtype=f32) / f32(32)))).astype(f32)
    inv8 = np.tile(inv, 8).reshape(1, 256).astype(f32)
    i_ = np.arange(128)[:, None]
    u_ = np.arange(TABW)[None, :]
    dd = i_ - u_ + 1408
    tabs = np.abs(dd).astype(f32)
    m = _mult(dd)
    tln = np.where(m > 0, np.log(np.maximum(m, 1.0)), NEG).astype(f32)
    slopes = (2.0 ** (-8.0 * np.arange(1, 17) / 16.0)).astype(f32)
    nslope = (-slopes).reshape(1, 16).astype(f32)
    ident = np.eye(128, dtype=f32)

    common = {
        "w_in": w_in_p, "w_uq": w_uq_p, "w_ukv": w_ukv_p, "w_out": w_out_, "w_ple": w_ple_, "w_pg": w_pg_,
        "g_mix": np.asarray(g_mix, f32).reshape(1, D), "g_ql": np.asarray(g_q_latent, f32).reshape(1, 1024),
        "g_kvl": np.asarray(g_kv_latent, f32).reshape(1, 512),
        "g_oa": np.ascontiguousarray(np.asarray(g_out_mla, f32).reshape(16, 128).T),
        "g_ob": np.ascontiguousarray(np.asarray(g_out_dil, f32).reshape(16, 128).T),
        "g_ple": np.asarray(g_ple, f32).reshape(1, D), "g_fin": np.asarray(g_final, f32).reshape(1, D),
        "c_inv8": inv8, "c_tabs": tabs, "c_tln": tln, "c_nslope": nslope, "c_ident": ident,
    }
    xpad = np.concatenate([np.zeros((T, D), f32), x, np.zeros((T, D), f32)], axis=0)
    in_maps = []
    for r in range(NCORES):
        mcore = dict(common)
        mcore["x_ext"] = np.ascontiguousarray(xpad[r * T:(r + 3) * T])
        mcore["p_in"] = np.ascontiguousarray(p[r * T:(r + 1) * T])
        mcore["pos_in"] = np.ascontiguousarray(positions[r * T:(r + 1) * T].reshape(NT, 128).T)
        mcore["c_edge"] = np.array([[NEG if r == 0 else 0.0, NEG if r == NCORES - 1 else 0.0, 0.0, 0.0]], f32)
        in_maps.append(mcore)
    return in_maps


def kernel(x, p, positions, g_mix, w_in, g_q_latent, w_uq, g_kv_latent, w_ukv,
           g_out_mla, g_out_dil, w_out, w_ple, g_ple, w_ple_gate, g_final):
    f32 = np.float32
    in_maps = _prep(x, p, positions, g_mix, w_in, g_q_latent, w_uq, g_kv_latent, w_ukv,
                    g_out_mla, g_out_dil, w_out, w_ple, g_ple, w_ple_gate, g_final)
    if "nc" not in _NC_CACHE:
        _NC_CACHE["nc"] = build_program()
    res = run_bass_kernel_spmd(_NC_CACHE["nc"], in_maps, core_ids=list(range(NCORES)))
    out = np.concatenate([np.asarray(res.results[r]["out"], f32) for r in range(NCORES)], axis=0)
    return out.reshape(1, S, D)
```
